# Optimizing a Trainium2 kernel written in Bass

```python
import jax, jax.numpy as jnp
from jax import lax
import numpy as np

D_MODEL = 1024
BATCH = 1
SEQ = 16384
DEPTH = 4

CHUNK = 64
N_A_LAYERS = DEPTH // 2
N_B_LAYERS = DEPTH - N_A_LAYERS
HEAD_DIM = 64
N_MIX_HEADS = (3 * D_MODEL // 4) // HEAD_DIM
D_MIX = N_MIX_HEADS * HEAD_DIM
N_MEM_HEADS = 4
D_MEMQ = D_MODEL - D_MIX
N_MEM = 256
LORA_W = 64
LORA_A = 64
LORA_V = 32
LORA_G = 128
LN_X_EPS = 64e-5
RMS_EPS = 1e-6
LEFT_CHUNKS = 8
BAND = (LEFT_CHUNKS + 1) * CHUNK
PAD_LEN = LEFT_CHUNKS * CHUNK
REL_CLIP = 256
REL_TABLE = CHUNK + REL_CLIP
D_FF = 256 * ((8 * D_MODEL // 3 + 255) // 256)
CONV_W = 3

kernel_name = 'hybrid_rwkv7_chunkattn_yoco_mem'


def rms_norm(x, g):
    xf = x.astype(jnp.float32)
    y = xf * lax.rsqrt(jnp.mean(xf * xf, axis=-1, keepdims=True) + RMS_EPS)
    return (y * g.astype(jnp.float32)).astype(x.dtype)


def token_shift(x):
    return jnp.pad(x[:, :-1], ((0, 0), (1, 0), (0, 0)))


def wkv7_scan(r, w, k, v, a, b):
    bsz, _, nh, hd = r.shape
    xs = tuple(jnp.swapaxes(t, 0, 1) for t in (r, w, k, v, a, b))

    def step(state, inp):
        r_t, w_t, k_t, v_t, a_t, b_t = inp
        sa = jnp.einsum('bhvk,bhk->bhv', state, a_t)
        state = (state * w_t[:, :, None, :] + sa[..., None] * b_t[:, :, None, :]
                 + v_t[..., None] * k_t[:, :, None, :])
        return state, jnp.einsum('bhvk,bhk->bhv', state, r_t)

    s0 = jnp.zeros((bsz, nh, hd, hd), jnp.float32)
    _, ys = lax.scan(step, s0, xs)
    return jnp.swapaxes(ys, 0, 1)


def rwkv7_time_mix(h, p_rkv, mu_rkv, mu_x, w0, w1, w2, a0, a1, a2, g1, g2,
                   k_k, k_a, r_k, lnx_w, lnx_b, v_first, v_res):
    bsz, seq, _ = h.shape
    f32 = jnp.float32
    dh = token_shift(h) - h
    xw = h + dh * mu_x[0]
    xa = h + dh * mu_x[1]
    xg = h + dh * mu_x[2]
    p_r, p_k, p_v = jnp.split(p_rkv, 3, axis=-1)
    r = p_r + (token_shift(p_r) - p_r) * mu_rkv[0]
    k = p_k + (token_shift(p_k) - p_k) * mu_rkv[1]
    v = p_v + (token_shift(p_v) - p_v) * mu_rkv[2]
    if v_res is None:
        v_first = v
    else:
        mu_v, v0, v1, v2 = v_res
        xv = h + dh * mu_v
        v = v + (v_first - v) * jax.nn.sigmoid(v0 + (xv @ v1) @ v2)
    w_log = (-jax.nn.softplus(-(w0 + jnp.tanh(xw @ w1) @ w2)) - 0.5).astype(f32)
    decay = jnp.exp(-jnp.exp(w_log))
    a = jax.nn.sigmoid(a0 + (xa @ a1) @ a2).astype(f32)
    g = jax.nn.sigmoid(xg @ g1) @ g2

    def heads(t):
        return t.astype(f32).reshape(bsz, seq, N_MIX_HEADS, HEAD_DIM)

    def per_head(p):
        return p.astype(f32).reshape(N_MIX_HEADS, HEAD_DIM)

    r_h, k_h, v_h, w_h, a_h = heads(r), heads(k), heads(v), heads(decay), heads(a)
    kk = k_h * per_head(k_k)
    kk = kk / jnp.maximum(jnp.sqrt(jnp.sum(kk * kk, axis=-1, keepdims=True)), 1e-12)
    k_h = k_h * (1.0 + (a_h - 1.0) * per_head(k_a))
    y = wkv7_scan(r_h, w_h, k_h, v_h, -kk, kk * a_h)
    mean = jnp.mean(y, axis=-1, keepdims=True)
    var = jnp.mean(jnp.square(y - mean), axis=-1, keepdims=True)
    y = (y - mean) * lax.rsqrt(var + LN_X_EPS) * per_head(lnx_w) + per_head(lnx_b)
    y = y + jnp.sum(r_h * k_h * r_k.astype(f32), axis=-1, keepdims=True) * v_h
    out = y.reshape(bsz, seq, D_MIX).astype(h.dtype) * g
    return out, v_first


def rel_bias_band(table):
    i = jnp.arange(CHUNK)[:, None]
    j = jnp.arange(BAND)[None, :]
    dist = i + PAD_LEN - j
    idx = jnp.clip(dist, -(CHUNK - 1), REL_CLIP) + (CHUNK - 1)
    return table[:, idx].astype(jnp.float32)


def chunk_attention(q, k_pad, v_pad, bias):
    bsz, nh, seq, hd = q.shape
    n_chunks = seq // CHUNK
    scale = HEAD_DIM ** -0.5
    neg = jnp.finfo(jnp.float32).min

    def one_chunk(c):
        start = c * CHUNK
        qc = lax.dynamic_slice_in_dim(q, start, CHUNK, axis=2)
        kc = lax.dynamic_slice_in_dim(k_pad, start, BAND, axis=2)
        vc = lax.dynamic_slice_in_dim(v_pad, start, BAND, axis=2)
        s = jnp.einsum('bhqd,bhkd->bhqk', qc, kc).astype(jnp.float32) * scale + bias
        kpos = start - PAD_LEN + jnp.arange(BAND)
        s = jnp.where(kpos >= 0, s, neg)
        p = jax.nn.softmax(s, axis=-1).astype(vc.dtype)
        return jnp.einsum('bhqk,bhkd->bhqd', p, vc)

    o = lax.map(one_chunk, jnp.arange(n_chunks))
    return o.transpose(1, 0, 3, 2, 4).reshape(bsz, seq, nh * hd)


def memory_attention(q_mem, mem_n, w_mem_kv):
    bsz, seq, _ = q_mem.shape
    k_m, v_m = jnp.split(mem_n @ w_mem_kv, 2, axis=-1)
    q = q_mem.reshape(bsz, seq, N_MEM_HEADS, HEAD_DIM)
    k = k_m.reshape(bsz, -1, N_MEM_HEADS, HEAD_DIM)
    v = v_m.reshape(bsz, -1, N_MEM_HEADS, HEAD_DIM)
    s = jnp.einsum('bshd,bmhd->bhsm', q, k).astype(jnp.float32) * HEAD_DIM ** -0.5
    p = jax.nn.softmax(s, axis=-1).astype(v.dtype)
    return jnp.einsum('bhsm,bmhd->bshd', p, v).reshape(bsz, seq, D_MEMQ)


def conv_glu(h, w_in, conv_w, conv_b, w_out):
    seq = h.shape[1]
    gate, val = jnp.split(h @ w_in, 2, axis=-1)
    gp = jnp.pad(gate, ((0, 0), (CONV_W - 1, 0), (0, 0)))
    conv = conv_b + sum(conv_w[j] * gp[:, j:j + seq] for j in range(CONV_W))
    return (jax.nn.gelu(conv, approximate=False) * val) @ w_out


def setup_inputs(seed: int = 0) -> dict:
    key = jax.random.key(seed)
    ks = iter(jax.random.split(key, 48))
    f32 = jnp.float32
    D = D_MODEL
    NA, NB = N_A_LAYERS, N_B_LAYERS
    NV = max(NA - 1, 0)

    def nrm(shape, scale):
        return jax.random.normal(next(ks), shape, f32) * scale

    def gain(shape):
        return 1.0 + nrm(shape, 0.02)

    def unif(shape):
        return jax.random.uniform(next(ks), shape, f32)

    n = jnp.arange(D_MIX, dtype=f32) / (D_MIX - 1)
    w0_base = -6.5 + 5.0 * n ** 0.85
    return {
        'x': nrm((BATCH, SEQ, D), 1.0),
        'mem': nrm((BATCH, N_MEM, D), 1.0),
        'mem_norm': gain((D,)),
        'ln1': gain((DEPTH, D)),
        'ln2': gain((DEPTH, D)),
        'w_out': nrm((DEPTH, D, D), D ** -0.5),
        'w_mem_kv': nrm((DEPTH, D, 2 * D_MEMQ), D ** -0.5),
        'ffn_in': nrm((DEPTH, D, 2 * D_FF), D ** -0.5),
        'ffn_conv': nrm((DEPTH, CONV_W, D_FF), CONV_W ** -0.5),
        'ffn_conv_b': nrm((DEPTH, D_FF), 0.02),
        'ffn_out': nrm((DEPTH, D_FF, D), D_FF ** -0.5),
        'a_w_in': nrm((NA, D, 3 * D_MIX + D_MEMQ), D ** -0.5),
        'a_mu_rkv': unif((NA, 3, D_MIX)),
        'a_mu_x': unif((NA, 3, D)),
        'a_w0': w0_base[None, :] + nrm((NA, D_MIX), 0.1),
        'a_w1': nrm((NA, D, LORA_W), D ** -0.5),
        'a_w2': nrm((NA, LORA_W, D_MIX), 0.1 * LORA_W ** -0.5),
        'a_a0': nrm((NA, D_MIX), 0.1),
        'a_a1': nrm((NA, D, LORA_A), D ** -0.5),
        'a_a2': nrm((NA, LORA_A, D_MIX), 0.1 * LORA_A ** -0.5),
        'a_g1': nrm((NA, D, LORA_G), D ** -0.5),
        'a_g2': nrm((NA, LORA_G, D_MIX), LORA_G ** -0.5),
        'a_k_k': 0.85 + nrm((NA, D_MIX), 0.02),
        'a_k_a': gain((NA, D_MIX)),
        'a_r_k': nrm((NA, N_MIX_HEADS, HEAD_DIM), 0.1),
        'a_lnx_w': gain((NA, D_MIX)),
        'a_lnx_b': nrm((NA, D_MIX), 0.02),
        'a_mu_v': unif((NV, D)),
        'a_v0': 1.0 + nrm((NV, D_MIX), 0.1),
        'a_v1': nrm((NV, D, LORA_V), D ** -0.5),
        'a_v2': nrm((NV, LORA_V, D_MIX), 0.1 * LORA_V ** -0.5),
        'ln_kv': gain((D,)),
        'w_kv': nrm((D, 2 * D_MIX), D ** -0.5),
        'b_w_in': nrm((NB, D, D_MIX + D_MEMQ), D ** -0.5),
        'b_rel': nrm((NB, N_MIX_HEADS, REL_TABLE), 0.2),
        'ln_f': gain((D,)),
    }


def reference(x, mem, mem_norm, ln1, ln2, w_out, w_mem_kv, ffn_in, ffn_conv, ffn_conv_b,
              ffn_out, a_w_in, a_mu_rkv, a_mu_x, a_w0, a_w1, a_w2, a_a0, a_a1, a_a2,
              a_g1, a_g2, a_k_k, a_k_a, a_r_k, a_lnx_w, a_lnx_b, a_mu_v, a_v0, a_v1, a_v2,
              ln_kv, w_kv, b_w_in, b_rel, ln_f):
    bsz, seq, _ = x.shape
    mem_n = rms_norm(mem, mem_norm)
    v_first = None
    k_pad = v_pad = None
    for layer in range(DEPTH):
        if layer < N_A_LAYERS:
            i = layer
            h = rms_norm(x, ln1[layer])
            p = h @ a_w_in[i]
            v_res = None if i == 0 else (a_mu_v[i - 1], a_v0[i - 1], a_v1[i - 1], a_v2[i - 1])
            mix, v_first = rwkv7_time_mix(
                h, p[..., :3 * D_MIX], a_mu_rkv[i], a_mu_x[i], a_w0[i], a_w1[i], a_w2[i],
                a_a0[i], a_a1[i], a_a2[i], a_g1[i], a_g2[i], a_k_k[i], a_k_a[i], a_r_k[i],
                a_lnx_w[i], a_lnx_b[i], v_first, v_res)
            q_mem = p[..., 3 * D_MIX:]
        else:
            j = layer - N_A_LAYERS
            if j == 0:
                k_s, v_s = jnp.split(rms_norm(x, ln_kv) @ w_kv, 2, axis=-1)

                def to_band(t):
                    t = t.reshape(bsz, seq, N_MIX_HEADS, HEAD_DIM).transpose(0, 2, 1, 3)
                    return jnp.pad(t, ((0, 0), (0, 0), (PAD_LEN, 0), (0, 0)))

                k_pad, v_pad = to_band(k_s), to_band(v_s)
            h = rms_norm(x, ln1[layer])
            p = h @ b_w_in[j]
            q = p[..., :D_MIX].reshape(bsz, seq, N_MIX_HEADS, HEAD_DIM).transpose(0, 2, 1, 3)
            mix = chunk_attention(q, k_pad, v_pad, rel_bias_band(b_rel[j]))
            q_mem = p[..., D_MIX:]
        mo = memory_attention(q_mem, mem_n, w_mem_kv[layer])
        x = x + jnp.concatenate([mix, mo], axis=-1) @ w_out[layer]
        x = x + conv_glu(rms_norm(x, ln2[layer]), ffn_in[layer], ffn_conv[layer],
                         ffn_conv_b[layer], ffn_out[layer])
    return rms_norm(x, ln_f)
```

```python
import os, contextlib
import numpy as np
import concourse.bass as bass
import concourse.mybir as mybir
from concourse.bass_utils import run_bass_kernel_spmd

F32 = mybir.dt.float32
BF16 = mybir.dt.bfloat16
I32 = mybir.dt.int32
AF = mybir.ActivationFunctionType
ALU = mybir.AluOpType


class Prog:
    COMPUTE = ('pe', 'act', 'dve', 'pool')
    NDS = 6
    _k = [0]
    SEM_ES = None
    GLOB = {}

    def __init__(self, nc):
        self.nc = nc
        self.ops = []
        self.last_w = {}
        self.reads = {}

    def op(self, eng, fn, reads=(), writes=(), kind='c'):
        i = len(self.ops)
        deps = {}
        for r in reads:
            j = self.last_w.get(r)
            if j is not None:
                deps[j] = deps.get(j, '') + 'R'
        for w in writes:
            j = self.last_w.get(w)
            if j is not None:
                deps[j] = deps.get(j, '') + 'W'
            for j in self.reads.get(w, ()):
                deps[j] = deps.get(j, '') + 'A'
        for r in reads:
            self.reads.setdefault(r, []).append(i)
        for w in writes:
            self.last_w[w] = i
            self.reads[w] = []
        o = dict(i=i, eng=eng, fn=fn, kind=kind, deps=[], signal=False)
        for j, why in deps.items():
            if j == i:
                continue
            pj = self.ops[j]
            if pj['kind'] == 'c' and pj['eng'] == eng:
                if eng == 'pe':
                    continue
                if set(why) <= {'A'}:
                    continue
            pj['signal'] = True
            o['deps'].append(j)
        self.ops.append(o)
        return i

    def emit(self):
        nc = self.nc
        engs = {'pe': nc.tensor, 'act': nc.scalar, 'dve': nc.vector, 'pool': nc.gpsimd, 'sp': nc.sync}
        import contextlib
        with contextlib.ExitStack() as es:
            Prog._k[0] += 1; kk = Prog._k[0]
            ses = Prog.SEM_ES if Prog.SEM_ES is not None else es
            G = Prog.GLOB
            if 'csem' not in G:
                G['csem'] = {e: ses.enter_context(nc.semaphore('gc_%s' % e)) for e in self.COMPUTE}
                G['dsem'] = {'sp': [ses.enter_context(nc.semaphore('gd_sp%d' % k)) for k in range(self.NDS)]}
                G['ccount'] = {e: 0 for e in self.COMPUTE}
                G['dcount'] = {'sp': 0}
            csem = G['csem']; dsem = G['dsem']
            ccount = G['ccount']
            dcount = G['dcount']
            base_c = dict(ccount)
            for o in self.ops:
                if o['kind'] == 'cc':
                    o['sem'] = ses.enter_context(nc.semaphore('cc%d_%d' % (kk, o['i'])))
                    o['val'] = 1
                    o['inc'] = None
                    o['prev_val'] = 0
                    o['signal'] = True
                elif o['kind'] == 'c':
                    if o['signal']:
                        ccount[o['eng']] += 1
                        o['sem'] = csem[o['eng']]
                        o['val'] = ccount[o['eng']]
                        o['inc'] = 1
                else:
                    k = dcount[o['eng']]
                    dcount[o['eng']] += 1
                    o['sem'] = dsem[o['eng']][k % self.NDS]
                    o['val'] = 16 * (k // self.NDS + 1)
                    o['inc'] = 16
                    o['prev_val'] = 16 * (k // self.NDS)
                    o['signal'] = True
            if os.environ.get('KDBG'):
                print('stage counts', ccount, dcount, len(self.ops))
            block = es.enter_context(nc.Block())
            per = {e: [o for o in self.ops if o['eng'] == e] for e in engs}
            ops = self.ops

            def body(ename):
                def f(eng):
                    known = {csem[e]: base_c[e] for e in csem}
                    for o in per[ename]:
                        waits = {}
                        for j in o['deps']:
                            pj = ops[j]
                            s = pj['sem']
                            waits[s] = max(waits.get(s, (0, None))[0], pj['val']), s
                        if o['kind'] != 'c' and o['prev_val'] > 0:
                            s = o['sem']
                            waits[s] = max(waits.get(s, (0, None))[0], o['prev_val']), s
                        for (v, s) in waits.values():
                            if known.get(s, 0) >= v:
                                continue
                            eng.wait_ge(s, v)
                            known[s] = v
                        ins = o['fn'](eng)
                        if o['signal']:
                            if o['inc'] is None:
                                ins.then_inc(o['sem'])
                            else:
                                ins.then_inc(o['sem'], o['inc'])
                    for o in per[ename]:
                        if o['kind'] != 'c':
                            if known.get(o['sem'], 0) < o['val']:
                                eng.wait_ge(o['sem'], o['val'])
                                known[o['sem']] = o['val']
                return f

            block.sync(body('sp'))
            block.tensor(body('pe'))
            block.scalar(body('act'))
            block.vector(body('dve'))
            block.gpsimd(body('pool'))


D = 1024; SEQ = 16384; NCORE = 8; TPC = SEQ // NCORE; TB = 256; NBLK = TPC // TB
DMIX = 768; DFF = 2816; NFF = DFF // 128
MMDT = F32
SUB = ALU.subtract; ADD = ALU.add; MUL = ALU.mult


class St:
    _n = [0]

    def __init__(s, nc, pbanks):
        s.nc = nc; s.P = Prog(nc); s.es = contextlib.ExitStack(); s.pb = pbanks; s.k = 0
        St._n[0] += 1; s.sid = St._n[0]

    def sb(s, name, shape, dt=F32):
        return s.es.enter_context(s.nc.sbuf_tensor('%s_s%d' % (name, s.sid), shape, dt))

    def close(s):
        s.P.emit()
        s.es.close()

    def dma(s, out, in_, r, w, q='sp'):
        s.P.op(q, lambda e: e.dma_start(out=out, in_=in_), reads=r, writes=w, kind='d')

    def mm(s, out, lhsT, rhs, r, w, start=True, stop=True):
        s.P.op('pe', lambda e: e.matmul(out, lhsT, rhs, start=start, stop=stop), reads=r, writes=w)

    def act(s, out, in_, func, r, w, **kw):
        s.P.op('act', lambda e: e.activation(out=out, in_=in_, func=func, **kw), reads=r, writes=w)

    def tt(s, eng, out, in0, in1, op, r, w):
        s.P.op(eng, lambda e: e.tensor_tensor(out=out, in0=in0, in1=in1, op=op), reads=r, writes=w)

    def ts(s, eng, out, in0, s1, s2, op0, op1, r, w):
        if s2 is None:
            s.P.op(eng, lambda e: e.tensor_scalar(out=out, in0=in0, scalar1=s1, scalar2=None, op0=op0), reads=r, writes=w)
        else:
            s.P.op(eng, lambda e: e.tensor_scalar(out=out, in0=in0, scalar1=s1, scalar2=s2, op0=op0, op1=op1), reads=r, writes=w)

    def stt(s, out, in0, scalar, in1, op0, op1, r, w):
        s.P.op('dve', lambda e: e.scalar_tensor_tensor(out=out, in0=in0, scalar=scalar, in1=in1, op0=op0, op1=op1), reads=r, writes=w)

    def copy(s, eng, out, in_, r, w):
        if eng == 'act':
            s.act(out, in_, AF.Copy, r, w)
        else:
            s.P.op(eng, lambda e: e.tensor_copy(out=out, in_=in_), reads=r, writes=w)

    def recip(s, out, in_, r, w):
        s.P.op('dve', lambda e: e.reciprocal(out=out, in_=in_), reads=r, writes=w)

    def memset(s, eng, out, val, w):
        s.P.op(eng, lambda e: e.memset(out, val), reads=(), writes=w)


def wview(w2d):
    return w2d.rearrange("(kc p) n -> p kc n", p=128)


def rmsnorm(st, C, x, xr, gcol, out, outr, N, sq, rs, eps=1e-6, nk=8):
    sqr = 'sq'; rsr = 'rs'
    st.act(sq[:, :, 0:N], x, AF.Square, [xr], [sqr])
    pb = st.pb[7]
    for k in range(nk):
        st.mm(pb[:, 0:N], C['ones'][:, 0:128], sq[:, k, 0:N], [sqr], ['pb7'], start=(k == 0), stop=(k == nk - 1))
    st.act(rs[:, 0:N], pb[:, 0:N], AF.Sqrt, ['pb7'], [rsr], scale=1.0 / (128 * nk), bias=eps)
    st.recip(rs[:, 0:N], rs[:, 0:N], [rsr], [rsr])
    for k in range(nk):
        st.stt(out[:, k, :], x[:, k, :], C['pv'][:, gcol + k:gcol + k + 1], rs[:, 0:N], MUL, MUL, [xr, rsr], [outr])


def linear(st, C, wv, c0, m, inp, inr, N, nk, bank, slot, bcol=0, bf=False):
    wt = C['wbuf'][slot]
    wr = 'wbuf%d' % slot
    st.dma(wt[:, 0:nk, 0:m], wv[:, :, c0:c0 + m], [], [wr])
    if bf:
        wb = C['wbf'][slot]; wbr = 'wbf%d' % slot
        st.copy('pool', wb[:, 0:nk, 0:m], wt[:, 0:nk, 0:m], [wr], [wbr])
        wt = wb; wr = wbr
    pb = st.pb[bank]
    for k in range(nk):
        st.mm(pb[0:m, bcol:bcol + N], wt[:, k, 0:m], inp[:, k, :], [wr, inr], ['pb%d' % bank], start=(k == 0), stop=(k == nk - 1))
    return pb


_xc = [0]


def exchange(st, C, pieces):
    nc = st.nc
    W = sum(p[4] for p in pieces)
    _xc[0] += 1
    mine = nc.dram_tensor("xmine%d" % _xc[0], [128, W], F32)
    gath = nc.dram_tensor("xgath%d" % _xc[0], [NCORE * 128, W], F32)
    off = 0
    mr = 'xmine%d' % _xc[0]; gr = 'xgath%d' % _xc[0]
    for (src, sr, dst, dr, w) in pieces:
        st.dma(mine[:, off:off + w], src, [sr], [mr])
        off += w
    st.P.op('pool', lambda e: e.collective_compute("AllGather", ALU.bypass, replica_groups=[list(range(NCORE))],
                                                   ins=[mine.ap().opt()], outs=[gath.ap().opt()]),
            reads=[mr], writes=[gr], kind='cc')
    off = 0
    wmax = max(p[4] for p in pieces)
    tmp = [st.sb('xtmp%d_%d' % (_xc[0], i), [128, wmax]) for i in range(2)]
    n = 0
    for (src, sr, dst, dr, w) in pieces:
        for j in range(NCORE - 1):
            t = tmp[n % 2]; tr = 'xtmp%d' % (n % 2); n += 1
            st.dma(t[:, 0:w], gath[j * 128:(j + 1) * 128, off:off + w], [gr], [tr])
            if j == 0:
                st.ts('dve', dst, t[:, 0:w], C['sel'][:, 0:1], None, MUL, None, [tr], [dr])
            else:
                st.stt(dst, t[:, 0:w], C['sel'][:, j:j + 1], dst, MUL, ADD, [tr, dr], [dr])
        off += w
    return gath, gr


def mem_kv(st, C, l, A):
    wv = wview(A['w_mem_kv'][l])
    for mc in range(2):
        pb = linear(st, C, wv, mc * 128, 128, C['memn'], 'memn', 256, 8, mc, mc)
        st.copy('act', C['kmT'][:, mc, :], pb[:, 0:256], ['pb%d' % mc], ['kmT'])
    wt = st.sb('wmv', [128, 8, 256])
    st.dma(wt[:], wv[:, :, 256:512], [], ['wmv'])
    for kt in range(2):
        pb = st.pb[2 + kt]
        for k in range(8):
            st.mm(pb[:, 0:256], C['memn'][:, k, kt * 128:(kt + 1) * 128], wt[:, k, :], ['wmv', 'memn'], ['pb%d' % (2 + kt)], start=(k == 0), stop=(k == 7))
        st.copy('act', C['vm'][:, kt, :], pb[:, 0:256], ['pb%d' % (2 + kt)], ['vm'])


def mem_attn(st, C, qm, qmr, mo, mor, T):
    for mh in range(4):
        mc = mh // 2; p0 = 64 * (mh % 2)
        for kt in range(2):
            st.mm(st.pb[6][:, kt * TB:(kt + 1) * TB], C['kmT'][p0:p0 + 64, mc, kt * 128:(kt + 1) * 128], qm[p0:p0 + 64, mc, :],
                  ['kmT', qmr], ['pb6'])
            st.act(T['PTm'][kt][:], st.pb[6][:, kt * TB:(kt + 1) * TB], AF.Exp, ['pb6'], ['PTm%d' % kt], scale=0.125)
        for kt in range(2):
            st.mm(st.pb[4][p0:p0 + 64, 0:TB], C['vm'][:, kt, mh * 64:(mh + 1) * 64], T['PTm'][kt][:], ['vm', 'PTm%d' % kt], ['pb4'],
                  start=(kt == 0), stop=(kt == 1))
        for kt in range(2):
            st.mm(st.pb[5][p0:p0 + 64, 0:TB], C['onesb'][:, 0:64], T['PTm'][kt][:], ['PTm%d' % kt], ['pb5'], start=(kt == 0), stop=(kt == 1))
        if mh % 2 == 1:
            st.recip(T['rdm'][:], st.pb[5][:, 0:TB], ['pb5'], ['rdm'])
            st.tt('dve', mo[:, mc, :], st.pb[4][:, 0:TB], T['rdm'][:], MUL, ['pb4', 'rdm'], [mor])


def out_proj(st, C, l, A, cat, catr, xblk, xr, xo, xor):
    wv = wview(A['w_out'][l])
    for d in range(8):
        slot = d % 4; bank = d % 2
        wt = C['wbuf'][slot]; wr = 'wbuf%d' % slot
        st.dma(wt[:, 0:8, 0:128], wv[:, :, d * 128:(d + 1) * 128], [], [wr])
        for kc in range(8):
            st.mm(st.pb[bank][:, 0:TB], wt[:, kc, :], cat[kc], [wr] + catr, ['pb%d' % bank], start=(kc == 0), stop=(kc == 7))
        st.tt('dve', xo[:, d, :], xblk[:, d, :], st.pb[bank][:, 0:TB], ADD, [xr, 'pb%d' % bank], [xor])


def ffn_stage(nc, C, A, l, xs, final_out=None, pub=False):
    st = St(nc, C['pb'])
    PV = C['PV']; pv = C['pv']
    xm = [st.sb('xm%d' % i, [128, 8, TB]) for i in range(2)]
    h2 = [st.sb('h2%d' % i, [128, 8, TB], BF16) for i in range(2)]
    hf = st.sb('hf', [128, 8, TB])
    sq = st.sb('sq', [128, 8, TB]); rs = st.sb('rs', [128, TB])
    u = st.sb('u', [128, NFF, TB], BF16)
    gprev = st.sb('gprev', [128, NFF, 2])
    hh = st.sb('hh', [128, 8, 2])
    wout = [st.sb('wout%d' % i, [128, NFF, 128]) for i in range(2)]
    woutb = [st.sb('woutb%d' % i, [128, NFF, 128], BF16) for i in range(2)]
    gsb = [st.sb('gsb%d' % i, [128, TB + 2]) for i in range(2)]
    cb = [st.sb('cb%d' % i, [128, TB]) for i in range(2)]
    ge = [st.sb('ge%d' % i, [128, TB]) for i in range(2)]
    winv = wview(A['ffn_in'][l])
    woutv = A['ffn_out'][l].rearrange("(j p) n -> p j n", p=128)
    xh = st.sb('xh', [128, 16])
    exchange(st, C, [(C['pubx'][:].rearrange("p a b -> p (a b)"), 'pubx', xh[:], 'xh', 16)])
    xh3 = xh[:].rearrange("p (a b) -> p a b", b=2)
    rmsnorm(st, C, xh3, 'xh', PV['ln2'] + 8 * l, hh, 'hh', 2, sq, rs)
    for j in range(NFF):
        pb = linear(st, C, winv, j * 128, 128, hh, 'hh', 2, 8, j % 2, j % 4)
        st.copy('act', gprev[:, j, :], pb[:, 0:2], ['pb%d' % (j % 2)], [('gprev', j)])
    for b in range(NBLK):
        x = xm[b % 2]; xr_ = 'xm%d' % (b % 2); h = h2[b % 2]; hr = 'h2%d' % (b % 2)
        st.dma(x[:], xs[:, :, b * TB:(b + 1) * TB], [('xs', b)], [xr_])
        rmsnorm(st, C, x[:], xr_, PV['ln2'] + 8 * l, h, hr, TB, sq, rs)
        for j in range(NFF):
            bg = (j % 2) * 2; bv = bg + 1
            linear(st, C, winv, j * 128, 128, h, hr, TB, 8, bg, (2 * j) % 4, bf=True)
            linear(st, C, winv, DFF + j * 128, 128, h, hr, TB, 8, bv, (2 * j + 1) % 4, bf=True)
            g = gsb[j % 2]; gr = 'gsb%d' % (j % 2); c = cb[j % 2]; cr = 'cb%d' % (j % 2)
            st.copy('pool', g[:, 0:2], gprev[:, j, :], [('gprev', j)], [gr])
            st.copy('act', g[:, 2:TB + 2], st.pb[bg][:, 0:TB], ['pb%d' % bg], [gr])
            st.copy('pool', gprev[:, j, :], g[:, TB:TB + 2], [gr], [('gprev', j)])
            cw = PV['conv_w'] + l * 3 * NFF
            st.ts('dve', c[:], g[:, 2:TB + 2], pv[:, cw + 2 * NFF + j:cw + 2 * NFF + j + 1], pv[:, PV['conv_b'] + l * NFF + j:PV['conv_b'] + l * NFF + j + 1],
                  MUL, ADD, [gr], [cr])
            st.stt(c[:], g[:, 1:TB + 1], pv[:, cw + NFF + j:cw + NFF + j + 1], c[:], MUL, ADD, [gr, cr], [cr])
            st.stt(c[:], g[:, 0:TB], pv[:, cw + j:cw + j + 1], c[:], MUL, ADD, [gr, cr], [cr])
            st.act(ge[j % 2][:], c[:], AF.Gelu, [cr], ['ge%d' % (j % 2)])
            st.tt('dve', u[:, j, :], ge[j % 2][:], st.pb[bv][:, 0:TB], MUL, ['ge%d' % (j % 2), 'pb%d' % bv], [('u', j)])
        for d in range(8):
            wt = wout[d % 2]; wr = 'wout%d' % (d % 2); bank = 4 + d % 2
            st.dma(wt[:], woutv[:, :, d * 128:(d + 1) * 128], [], [wr])
            wtb = woutb[d % 2]; wbr = 'woutb%d' % (d % 2)
            st.copy('pool', wtb[:], wt[:], [wr], [wbr])
            for j in range(NFF):
                st.mm(st.pb[bank][:, 0:TB], wtb[:, j, :], u[:, j, :], [wbr, ('u', j)], ['pb%d' % bank], start=(j == 0), stop=(j == NFF - 1))
            st.tt('dve', x[:, d, :], x[:, d, :], st.pb[bank][:, 0:TB], ADD, [xr_, 'pb%d' % bank], [xr_])
        if pub and b == NBLK - 1:
            st.copy('pool', C['pubx'][:], x[:, :, TB - 2:TB], [xr_], ['pubx'])
        if final_out is not None:
            rmsnorm(st, C, x[:], xr_, PV['ln_f'], hf, 'hf', TB, sq, rs)
            st.dma(final_out[:, :, b * TB:(b + 1) * TB], hf[:], ['hf'], [('fo', b)])
        else:
            st.dma(xs[:, :, b * TB:(b + 1) * TB], x[:], [xr_], [('xs', b)])
    st.close()


def kv_stage(nc, C, A, xs, DR):
    st = St(nc, C['pb']); PV = C['PV']
    xb = [st.sb('xb%d' % i, [128, 8, TB]) for i in range(2)]
    h = [st.sb('h%d' % i, [128, 8, TB]) for i in range(2)]
    sq = st.sb('sq', [128, 8, TB]); rs = st.sb('rs', [128, TB])
    wv = wview(A['w_kv'])
    wV = st.sb('wV', [128, 8, DMIX])
    st.dma(wV[:], wv[:, :, DMIX:2 * DMIX], [], ['wV'])
    Kb = [st.sb('Kb%d' % i, [128, 6, TB]) for i in range(2)]
    Vb = [st.sb('Vb%d' % i, [128, 2, 832]) for i in range(2)]
    Kl = st.sb('Kl', [128, 6, 512]); Vl = st.sb('Vl', [128, 4, 832])
    Kh = st.sb('Kh', [128, 6, 512]); Vh = st.sb('Vh', [128, 4, 832])
    for i in range(2):
        st.memset('pool', Vb[i][:, :, 768:832], 1.0, ['Vb%d' % i])
    for b in range(NBLK):
        x = xb[b % 2]; xr_ = 'xb%d' % (b % 2); hb = h[b % 2]; hr = 'h%d' % (b % 2)
        K = Kb[b % 2]; Kr = 'Kb%d' % (b % 2); V = Vb[b % 2]; Vr = 'Vb%d' % (b % 2)
        st.dma(x[:], xs[:, :, b * TB:(b + 1) * TB], [], [xr_])
        rmsnorm(st, C, x[:], xr_, PV['ln_kv'], hb, hr, TB, sq, rs)
        for hp in range(6):
            pb = linear(st, C, wv, hp * 128, 128, hb, hr, TB, 8, hp % 2, hp % 4)
            st.copy('act', K[:, hp, :], pb[:, 0:TB], ['pb%d' % (hp % 2)], [Kr])
        for tt in range(2):
            for hf in range(2):
                bank = 2 + hf
                for k in range(8):
                    st.mm(st.pb[bank][:, 0:384], hb[:, k, tt * 128:(tt + 1) * 128], wV[:, k, hf * 384:(hf + 1) * 384], ['wV', hr], ['pb%d' % bank],
                          start=(k == 0), stop=(k == 7))
                st.copy('act' if hf == 0 else 'dve', V[:, tt, hf * 384:(hf + 1) * 384], st.pb[bank][:, 0:384], ['pb%d' % bank], [Vr])
        st.dma(DR['KTd'][:, :, 512 + b * TB:512 + (b + 1) * TB], K[:], [Kr], [('KTd', b)])
        st.dma(DR['Vtd'][:, 4 + 2 * b:6 + 2 * b, :], V[:], [Vr], [('Vtd', b)])
        if b >= NBLK - 2:
            o = b - (NBLK - 2)
            st.copy('pool', Kl[:, :, o * TB:(o + 1) * TB], K[:], [Kr], ['Kl'])
            st.copy('pool', Vl[:, 2 * o:2 * o + 2, :], V[:], [Vr], ['Vl'])
    pieces = [(Kl[:, hp, :], 'Kl', Kh[:, hp, :], ('Kh', hp), 512) for hp in range(6)] + \
             [(Vl[:, t, :], 'Vl', Vh[:, t, :], ('Vh', t), 832) for t in range(4)]
    exchange(st, C, pieces)
    st.dma(DR['KTd'][:, :, 0:512], Kh[:], [('Kh', hp) for hp in range(6)], ['KTdh'])
    st.dma(DR['Vtd'][:, 0:4, :], Vh[:], [('Vh', t) for t in range(4)], ['Vtdh'])
    st.close()


def b_stage(nc, C, A, l, xs, DR):
    j = l - 2
    st = St(nc, C['pb']); PV = C['PV']
    xb = [st.sb('xb%d' % i, [128, 8, TB]) for i in range(2)]
    h = st.sb('h', [128, 8, TB])
    sq = st.sb('sq', [128, 8, TB]); rs = st.sb('rs', [128, TB])
    qT = st.sb('qT', [128, 6, TB]); qm = st.sb('qm', [128, 2, TB], BF16); att = st.sb('att', [128, 6, TB]); mo = st.sb('mo', [128, 2, TB])
    KTw = [st.sb('KTw%d' % i, [128, 6, TB + 512]) for i in range(1)]
    Vw = [st.sb('Vw%d' % i, [128, 6, 832]) for i in range(1)]
    EB = st.sb('EB', [128, 12, 640])
    Et = [st.sb('Et%d' % i, [128, 640]) for i in range(2)]
    Pm = [st.sb('Pm%d' % i, [128, 640]) for i in range(2)]
    rd = st.sb('rd', [128, 128])
    T = dict(PTm=[st.sb('PTm%d' % i, [128, TB], BF16) for i in range(2)], rdm=st.sb('rdm', [128, TB]))
    mem_kv(st, C, l, A)
    for hd in range(12):
        st.dma(Et[hd % 2][:], A['biasT'][j, hd], [], ['Et%d' % (hd % 2)])
        st.act(EB[:, hd, :], Et[hd % 2][:], AF.Exp, ['Et%d' % (hd % 2)], [('EB', hd)])
    wv = wview(A['b_w_in'][j])
    cnt = 0
    for b in range(NBLK):
        x = xb[b % 2]; xr_ = 'xb%d' % (b % 2)
        Kw = KTw[0]; Kr = 'KTw0'; V = Vw[0]; Vr = 'Vw0'
        st.dma(x[:], xs[:, :, b * TB:(b + 1) * TB], [('xs', b)], [xr_])
        st.dma(Kw[:], DR['KTd'][:, :, b * TB:b * TB + TB + 512], [], [Kr])
        st.dma(V[:], DR['Vtd'][:, 2 * b:2 * b + 6, :], [], [Vr])
        rmsnorm(st, C, x[:], xr_, PV['ln1'] + 8 * l, h, 'h', TB, sq, rs)
        for c in range(8):
            pb = linear(st, C, wv, c * 128, 128, h, 'h', TB, 8, c % 2, c % 4)
            if c < 6:
                st.copy('act', qT[:, c, :], pb[:, 0:TB], ['pb%d' % (c % 2)], [('qT', c)])
            else:
                st.copy('act', qm[:, c - 6, :], pb[:, 0:TB], ['pb%d' % (c % 2)], ['qm'])
        for tt in range(2):
            for hp in range(6):
                for e in range(2):
                    hd = 2 * hp + e; p0 = 64 * e
                    ba = 2 * (cnt % 2); bb = ba + 1
                    E = Et[cnt % 2]; Er = 'Et%d' % (cnt % 2); Pq = Pm[cnt % 2]; Pr = 'Pm%d' % (cnt % 2)
                    cnt += 1
                    for kt in range(5):
                        kc = tt * 128 + kt * 128
                        if kt < 4:
                            o = st.pb[ba][:, kt * 128:(kt + 1) * 128]; orr = 'pb%d' % ba
                        else:
                            o = st.pb[bb][:, 0:128]; orr = 'pb%d' % bb
                        st.mm(o, Kw[p0:p0 + 64, hp, kc:kc + 128], qT[p0:p0 + 64, hp, tt * 128:(tt + 1) * 128], [Kr, ('qT', hp)], [orr])
                    st.act(E[:, 0:512], st.pb[ba][:, 0:512], AF.Exp, ['pb%d' % ba], [Er], scale=0.125)
                    st.act(E[:, 512:640], st.pb[bb][:, 0:128], AF.Exp, ['pb%d' % bb], [Er], scale=0.125)
                    st.tt('pool', Pq[:], E[:], EB[:, hd, :], MUL, [Er, ('EB', hd)], [Pr])
                    for kt in range(5):
                        st.mm(st.pb[4][p0:p0 + 64, 0:128], V[:, tt + kt, hd * 64:(hd + 1) * 64], Pq[:, kt * 128:(kt + 1) * 128], [Vr, Pr], ['pb4'],
                              start=(kt == 0), stop=(kt == 4))
                    for kt in range(5):
                        st.mm(st.pb[5][p0:p0 + 64, 0:128], V[:, tt + kt, 768:832], Pq[:, kt * 128:(kt + 1) * 128], [Vr, Pr], ['pb5'],
                              start=(kt == 0), stop=(kt == 4))
                st.recip(rd[:], st.pb[5][:, 0:128], ['pb5'], ['rd'])
                st.tt('dve', att[:, hp, tt * 128:(tt + 1) * 128], st.pb[4][:, 0:128], rd[:], MUL, ['pb4', 'rd'], [('att', hp)])
        mem_attn(st, C, qm, 'qm', mo, 'mo', T)
        cat = [att[:, c, :] for c in range(6)] + [mo[:, c, :] for c in range(2)]
        out_proj(st, C, l, A, cat, [('att', c) for c in range(6)] + ['mo'], x, xr_, x, xr_)
        if b == NBLK - 1:
            st.copy('pool', C['pubx'][:], x[:, :, TB - 2:TB], [xr_], ['pubx'])
        st.dma(xs[:, :, b * TB:(b + 1) * TB], x[:], [xr_], [('xs', b)])
    st.close()


def a1_stage(nc, C, A, i, xsrc, DR):
    l = i
    st = St(nc, C['pb']); PV = C['PV']; pv = C['pv']
    sb = st.sb
    xb = [sb('xb%d' % k, [128, 8, TB]) for k in range(2)]
    hext = sb('hext', [128, 8, TB + 1]); dh = sb('dh', [128, 8, TB]); xz = sb('xz', [128, 8, TB])
    sq = sb('sq', [128, 8, TB]); rs = sb('rs', [128, TB])
    hh = sb('hh', [128, 8, 2]); xh = sb('xh', [128, 16]); hlast = sb('hlast', [128, 8, 1]); pprev = sb('pprev', [128, 18, 1])
    w1t = sb('w1t', [128, 8, 64]); a1t = sb('a1t', [128, 8, 64]); g1t = sb('g1t', [128, 8, 128]); v1t = sb('v1t', [128, 8, 32])
    w2t = sb('w2t', [64, DMIX]); a2t = sb('a2t', [64, DMIX]); g2t = sb('g2t', [128, DMIX]); v2t = sb('v2t', [32, DMIX])
    hz = dict(w=sb('hzw', [64, TB]), a=sb('hza', [64, TB]), g=sb('hzg', [128, TB]), v=sb('hzv', [32, TB]))
    names = ['pr', 'pk', 'pvv']
    pext = [sb('pext%d' % k, [128, TB + 1]) for k in range(3)]
    tl = {n: sb('t_' + n, [128, TB]) for n in ['d', 'r', 'k', 'v', 'sg', 'cs', 'ex', 'Wexp', 'Winv', 'Wexcl', 'ah', 'kk', 'kk2', 'rn', 'kkn', 'tm',
                                                'km', 'bb', 'rk2', 'vg', 'vf']}
    for n in ['rt', 'at', 'bt', 'kt', 'vb']:
        tl[n] = sb('t_' + n, [128, TB], BF16)
    NM = sb('NM', [128, 512], BF16); Am = [sb('Am%d' % k, [128, 128], BF16) for k in range(2)]; Nm = [sb('Nm%d' % k, [128, 128], BF16) for k in range(2)]
    TK = sb('TK', [128, 256], BF16); X = sb('X', [128, 128], BF16)
    PTb = [sb('PTb%d' % k, [128, 6, 4, 64]) for k in range(1)]; Qb = [sb('Qb%d' % k, [128, 6, 4, 64]) for k in range(1)]
    Rpb = [sb('Rpb%d' % k, [128, 6, TB]) for k in range(1)]; Ylb = [sb('Ylb%d' % k, [128, 6, TB]) for k in range(1)]
    bvb = [sb('bvb%d' % k, [128, 6, TB]) for k in range(1)]; gb = [sb('gb%d' % k, [128, 6, TB]) for k in range(1)]
    vfb = [sb('vfb%d' % k, [128, 6, TB]) for k in range(1)]
    qm = sb('qm', [128, 2, TB], BF16); mob = [sb('mob%d' % k, [128, 2, TB]) for k in range(2)]
    T = dict(PTm=[sb('PTm%d' % k, [128, TB], BF16) for k in range(2)], rdm=sb('rdm', [128, TB]))
    mem_kv(st, C, l, A)
    st.dma(w1t[:], wview(A['a_w1'][i]), [], ['w1t']); st.dma(a1t[:], wview(A['a_a1'][i]), [], ['a1t'])
    st.dma(g1t[:], wview(A['a_g1'][i]), [], ['g1t'])
    st.dma(w2t[:], A['a_w2'][i], [], ['w2t']); st.dma(a2t[:], A['a_a2'][i], [], ['a2t']); st.dma(g2t[:], A['a_g2'][i], [], ['g2t'])
    if i == 1:
        st.dma(v1t[:], wview(A['a_v1'][0]), [], ['v1t']); st.dma(v2t[:], A['a_v2'][0], [], ['v2t'])
    wv = wview(A['a_w_in'][i])
    if i == 0:
        st.dma(xh[:], A['xhalo'], [], ['xh'])
    else:
        exchange(st, C, [(C['pubx'][:].rearrange("p a b -> p (a b)"), 'pubx', xh[:], 'xh', 16)])
    rmsnorm(st, C, xh[:].rearrange("p (a b) -> p a b", b=2), 'xh', PV['ln1'] + 8 * l, hh, 'hh', 2, sq, rs)
    for oc in range(18):
        pb = linear(st, C, wv, oc * 128, 128, hh, 'hh', 2, 8, oc % 2, oc % 4)
        st.copy('act', pprev[:, oc, :], pb[:, 1:2], ['pb%d' % (oc % 2)], [('pprev', oc)])
    st.copy('pool', hlast[:], hh[:, :, 1:2], ['hh'], ['hlast'])
    mux = PV['mu_x'] + i * 24
    loras = [('w', w1t, 'w1t', 64, mux, AF.Tanh), ('a', a1t, 'a1t', 64, mux + 8, AF.Copy), ('g', g1t, 'g1t', 128, mux + 16, AF.Sigmoid)]
    if i == 1:
        loras.append(('v', v1t, 'v1t', 32, PV['mu_v'], AF.Copy))
    cnt = 0
    CUT = float(os.environ.get('KA1CUT', '99'))
    for b in range(NBLK if CUT > 1 else 0):
        x = xb[b % 2]; xr_ = 'xb%d' % (b % 2)
        PTt = PTb[0]; PTr = 'PTb0'; Qt = Qb[0]; Qr = 'Qb0'
        Rp = Rpb[0]; Rr = 'Rpb0'; Yl = Ylb[0]; Yr = 'Ylb0'
        bvt = bvb[0]; bvr = 'bvb0'; gt = gb[0]; gr = 'gb0'; vft = vfb[0]; vfr = 'vfb0'
        mo = mob[b % 2]; mor = 'mob%d' % (b % 2)
        st.dma(x[:], xsrc[:, :, b * TB:(b + 1) * TB], [], [xr_])
        if i == 1:
            st.dma(vft[:], DR['vfd'][:, :, b * TB:(b + 1) * TB], [], [vfr])
        hv = hext[:, :, 1:TB + 1]
        st.copy('pool', hext[:, :, 0:1], hlast[:], ['hlast'], ['hext'])
        rmsnorm(st, C, x[:], xr_, PV['ln1'] + 8 * l, hv, 'hext', TB, sq, rs)
        st.copy('pool', hlast[:], hext[:, :, TB:TB + 1], ['hext'], ['hlast'])
        st.tt('pool', dh[:], hext[:, :, 0:TB], hext[:, :, 1:TB + 1], SUB, ['hext'], ['dh'])
        for (z, wt, wr, L, mucol, fn) in loras:
            for k in range(8):
                st.stt(xz[:, k, :], dh[:, k, :], pv[:, mucol + k:mucol + k + 1], hext[:, k, 1:TB + 1], MUL, ADD, ['dh', 'hext'], ['xz'])
            for k in range(8):
                st.mm(st.pb[2][0:L, 0:TB], wt[:, k, 0:L], xz[:, k, :], [wr, 'xz'], ['pb2'], start=(k == 0), stop=(k == 7))
            st.act(hz[z][:], st.pb[2][0:L, 0:TB], fn, ['pb2'], ['hz' + z])
        if CUT <= 2:
            continue
        for c in range(2):
            pb = linear(st, C, wv, 2304 + c * 128, 128, hv, 'hext', TB, 8, c % 2, c % 4)
            st.copy('act', qm[:, c, :], pb[:, 0:TB], ['pb%d' % (c % 2)], ['qm'])
        mem_attn(st, C, qm, 'qm', mo, mor, T)
        st.dma(DR['mod'][:, :, b * TB:(b + 1) * TB], mo[:], [mor], [('mod', b)])
        for hp in range(6 if CUT > 3 else 0):
            t = tl
            for n3, (nm, oc) in enumerate([('r', hp), ('k', 6 + hp), ('v', 12 + hp)]):
                pe_ = pext[n3]; per = 'pext%d' % n3
                pb = linear(st, C, wv, oc * 128, 128, hv, 'hext', TB, 8, n3 % 2, (n3 + hp) % 4)
                st.copy('pool', pe_[:, 0:1], pprev[:, oc, :], [('pprev', oc)], [per])
                st.copy('act', pe_[:, 1:TB + 1], pb[:, 0:TB], ['pb%d' % (n3 % 2)], [per])
                st.copy('pool', pprev[:, oc, :], pe_[:, TB:TB + 1], [per], [('pprev', oc)])
                st.tt('pool', t['d'][:], pe_[:, 0:TB], pe_[:, 1:TB + 1], SUB, [per], ['t_d'])
                mc_ = PV['mu_rkv'] + i * 18 + n3 * 6 + hp
                st.stt(t[nm][:], t['d'][:], pv[:, mc_:mc_ + 1], pe_[:, 1:TB + 1], MUL, ADD, ['t_d', per], ['t_' + nm])
            if i == 0:
                st.copy('pool', vft[:, hp, :], t['v'][:], ['t_v'], [vfr])
            else:
                st.mm(st.pb[3][:, 0:TB], v2t[:, hp * 128:(hp + 1) * 128], hz['v'][:], ['v2t', 'hzv'], ['pb3'])
                c0 = PV['v0'] + hp
                st.act(t['vg'][:], st.pb[3][:, 0:TB], AF.Sigmoid, ['pb3'], ['t_vg'], bias=pv[:, c0:c0 + 1])
                st.tt('pool', t['d'][:], vft[:, hp, :], t['v'][:], SUB, [vfr, 't_v'], ['t_d'])
                st.tt('pool', t['d'][:], t['d'][:], t['vg'][:], MUL, ['t_d', 't_vg'], ['t_d'])
                st.tt('pool', t['v'][:], t['v'][:], t['d'][:], ADD, ['t_v', 't_d'], ['t_v'])
            st.mm(st.pb[3][:, 0:TB], w2t[:, hp * 128:(hp + 1) * 128], hz['w'][:], ['w2t', 'hzw'], ['pb3'])
            c0 = PV['w0'] + i * 6 + hp
            st.act(t['sg'][:], st.pb[3][:, 0:TB], AF.Sigmoid, ['pb3'], ['t_sg'], bias=pv[:, c0:c0 + 1])
            st.P.op('dve', lambda e, o=t['cs'], d1=t['sg']: e.tensor_tensor_scan(out=o[:], data0=C['cmask'][:, 0:TB], data1=d1[:], initial=0.0,
                                                                                 op0=MUL, op1=ADD), reads=['t_sg'], writes=['t_cs'])
            DK = 0.6065306597126334
            st.act(t['Wexp'][:], t['cs'][:], AF.Exp, ['t_cs'], ['t_Wexp'], scale=-DK)
            st.act(t['Winv'][:], t['cs'][:], AF.Exp, ['t_cs'], ['t_Winv'], scale=DK)
            st.tt('pool', t['ex'][:], t['cs'][:], t['sg'][:], SUB, ['t_cs', 't_sg'], ['t_ex'])
            st.act(t['Wexcl'][:], t['ex'][:], AF.Exp, ['t_ex'], ['t_Wexcl'], scale=-DK)
            for cc in range(TB // 64):
                st.copy('pool', C['WC'][:, hp, b * 4 + cc:b * 4 + cc + 1], t['Wexp'][:, cc * 64 + 63:cc * 64 + 64], ['t_Wexp'], [('WC', hp, b)])
            st.mm(st.pb[3][:, 0:TB], a2t[:, hp * 128:(hp + 1) * 128], hz['a'][:], ['a2t', 'hza'], ['pb3'])
            c0 = PV['a0'] + i * 6 + hp
            st.act(t['ah'][:], st.pb[3][:, 0:TB], AF.Sigmoid, ['pb3'], ['t_ah'], bias=pv[:, c0:c0 + 1])
            st.mm(st.pb[3][:, 0:TB], g2t[:, hp * 128:(hp + 1) * 128], hz['g'][:], ['g2t', 'hzg'], ['pb3'])
            st.copy('act', gt[:, hp, :], st.pb[3][:, 0:TB], ['pb3'], [gr])
            c0 = PV['k_k'] + i * 6 + hp
            st.ts('pool', t['kk'][:], t['k'][:], pv[:, c0:c0 + 1], None, MUL, None, ['t_k'], ['t_kk'])
            st.tt('pool', t['kk2'][:], t['kk'][:], t['kk'][:], MUL, ['t_kk'], ['t_kk2'])
            st.mm(st.pb[3][:, 0:TB], C['bones'][:, 0:128], t['kk2'][:], ['t_kk2'], ['pb3'])
            st.act(t['rn'][:], st.pb[3][:, 0:TB], AF.Sqrt, ['pb3'], ['t_rn'])
            st.ts('dve', t['rn'][:], t['rn'][:], 1e-12, None, ALU.max, None, ['t_rn'], ['t_rn'])
            st.recip(t['rn'][:], t['rn'][:], ['t_rn'], ['t_rn'])
            st.tt('pool', t['kkn'][:], t['kk'][:], t['rn'][:], MUL, ['t_kk', 't_rn'], ['t_kkn'])
            c0 = PV['k_a'] + i * 6 + hp
            st.ts('dve', t['tm'][:], t['ah'][:], -1.0, pv[:, c0:c0 + 1], ADD, MUL, ['t_ah'], ['t_tm'])
            st.stt(t['km'][:], t['tm'][:], 1.0, t['k'][:], ADD, MUL, ['t_tm', 't_k'], ['t_km'])
            st.tt('pool', t['bb'][:], t['kkn'][:], t['ah'][:], MUL, ['t_kkn', 't_ah'], ['t_bb'])
            c0 = PV['r_k'] + i * 6 + hp
            st.stt(t['rk2'][:], t['r'][:], pv[:, c0:c0 + 1], t['km'][:], MUL, MUL, ['t_r', 't_km'], ['t_rk2'])
            st.mm(st.pb[3][:, 0:TB], C['bones'][:, 0:128], t['rk2'][:], ['t_rk2'], ['pb3'])
            st.tt('dve', bvt[:, hp, :], st.pb[3][:, 0:TB], t['v'][:], MUL, ['pb3', 't_v'], [bvr])
            st.tt('pool', t['rt'][:], t['r'][:], t['Wexp'][:], MUL, ['t_r', 't_Wexp'], ['t_rt'])
            st.stt(t['at'][:], t['kkn'][:], -1.0, t['Wexcl'][:], MUL, MUL, ['t_kkn', 't_Wexcl'], ['t_at'])
            st.tt('pool', t['bt'][:], t['bb'][:], t['Winv'][:], MUL, ['t_bb', 't_Winv'], ['t_bt'])
            st.tt('pool', t['kt'][:], t['km'][:], t['Winv'][:], MUL, ['t_km', 't_Winv'], ['t_kt'])
            st.copy('pool', t['vb'][:], t['v'][:], ['t_v'], ['t_vb'])
            opr = ['t_rt', 't_at', 't_bt', 't_kt', 't_vb']
            for tt in range(TB // 128 if CUT > 4 else 0):
                for e in range(2):
                    p0 = 64 * e; cs_ = slice(tt * 128, (tt + 1) * 128)
                    at = t['at'][p0:p0 + 64, cs_]; bt = t['bt'][p0:p0 + 64, cs_]; kt = t['kt'][p0:p0 + 64, cs_]; rt = t['rt'][p0:p0 + 64, cs_]
                    vt = t['vb'][p0:p0 + 64, cs_]; I64 = C['identb'][p0:p0 + 64, p0:p0 + 64]
                    b1 = cnt % 2; b2 = 2 + cnt % 2; cnt += 1
                    M1 = st.pb[b1]; M1r = 'pb%d' % b1; M2 = st.pb[b2]; M2r = 'pb%d' % b2
                    st.mm(M1[:, 0:128], bt, at, opr, [M1r]); st.mm(M1[:, 128:256], bt, rt, opr, [M1r])
                    st.mm(M1[:, 256:384], kt, at, opr, [M1r]); st.mm(M1[:, 384:512], kt, rt, opr, [M1r])
                    st.tt('dve', NM[:], M1[:, 0:512], C['mask4'][:, 0:512], MUL, [M1r], ['NM'])
                    if CUT <= 4.3:
                        continue
                    st.mm(M2[:, 0:128], at, bt, opr, [M2r])
                    if CUT <= 4.5:
                        continue
                    st.mm(M2[:, 128:192], at, I64, opr, [M2r]); st.mm(M2[:, 192:256], vt, I64, opr, [M2r])
                    st.mm(M2[:, 256:320], bt, I64, opr, [M2r]); st.mm(M2[:, 320:384], kt, I64, opr, [M2r])
                    if CUT <= 4.6:
                        continue
                    st.tt('dve', Am[0][:], M2[:, 0:128], C['mlow'][:, 0:128], MUL, [M2r], ['Am0'])
                    if CUT <= 4.7:
                        continue
                    st.copy('dve', TK[:], M2[:, 128:384], [M2r], ['TK'])
                    if CUT <= 5:
                        continue
                    st.mm(st.pb[4][:, 0:64], NM[:, 256:384], TK[:, 64:128], ['NM', 'TK'], ['pb4'])
                    st.copy('dve', X[:, 0:64], TK[:, 0:64], ['TK'], ['X'])
                    st.copy('dve', X[:, 64:128], st.pb[4][:, 0:64], ['pb4'], ['X'])
                    st.copy('pool', Nm[0][:], NM[:, 0:128], ['NM'], ['Nm0'])
                    for it in range(6):
                        a_ = Am[it % 2]; ar = 'Am%d' % (it % 2); n_ = Nm[it % 2]; nr = 'Nm%d' % (it % 2)
                        st.mm(st.pb[4][:, 0:128], n_[:], X[:], [nr, 'X'], ['pb4'])
                        st.tt('dve', X[:], X[:], st.pb[4][:, 0:128], ADD, ['X', 'pb4'], ['X'])
                        if it < 5:
                            a2_ = Am[(it + 1) % 2]; a2r = 'Am%d' % ((it + 1) % 2); n2_ = Nm[(it + 1) % 2]; n2r = 'Nm%d' % ((it + 1) % 2)
                            st.mm(st.pb[5][:, 0:128], a_[:], n_[:], [ar, nr], ['pb5'])
                            st.mm(st.pb[5][:, 128:256], n_[:], a_[:], [ar, nr], ['pb5'])
                            st.copy('dve', n2_[:], st.pb[5][:, 0:128], ['pb5'], [n2r])
                            st.copy('dve', a2_[:], st.pb[5][:, 128:256], ['pb5'], [a2r])
                    if CUT <= 6:
                        continue
                    for cc in range(2):
                        rs_ = slice(64 * cc, 64 * cc + 64); ci = tt * 2 + cc
                        st.mm(st.pb[6][p0:p0 + 64, cc * 64:cc * 64 + 64], X[rs_, 0:64], TK[rs_, 128:192], ['X', 'TK'], ['pb6'])
                        st.tt('dve', PTt[p0:p0 + 64, hp, ci, :], st.pb[6][p0:p0 + 64, cc * 64:cc * 64 + 64], I64, ADD, ['pb6'], [PTr])
                        st.mm(st.pb[6][p0:p0 + 64, 128 + cc * 64:192 + cc * 64], TK[rs_, 128:192], X[rs_, 64:128], ['X', 'TK'], ['pb6'], start=True, stop=False)
                        st.mm(st.pb[6][p0:p0 + 64, 128 + cc * 64:192 + cc * 64], TK[rs_, 192:256], TK[rs_, 64:128], ['X', 'TK'], ['pb6'], start=False, stop=True)
                        gc = tt * 128 + cc * 64 + 63
                        st.ts('dve', Qt[p0:p0 + 64, hp, ci, :], st.pb[6][p0:p0 + 64, 128 + cc * 64:192 + cc * 64], t['Wexp'][p0:p0 + 64, gc:gc + 1], None,
                              MUL, None, ['pb6', 't_Wexp'], [Qr])
                    st.mm(st.pb[6][p0:p0 + 64, 256:384], X[:, 0:64], NM[:, 128:256], ['X', 'NM'], ['pb6'])
                    st.tt('dve', Rp[p0:p0 + 64, hp, cs_], st.pb[6][p0:p0 + 64, 256:384], rt, ADD, ['pb6', 't_rt'], [Rr])
                    st.mm(st.pb[6][p0:p0 + 64, 384:512], X[:, 64:128], NM[:, 128:256], ['X', 'NM'], ['pb6'], start=True, stop=False)
                    st.mm(st.pb[6][p0:p0 + 64, 384:512], TK[:, 64:128], NM[:, 384:512], ['TK', 'NM'], ['pb6'], start=False, stop=True)
                    st.copy('dve', Yl[p0:p0 + 64, hp, cs_], st.pb[6][p0:p0 + 64, 384:512], ['pb6'], [Yr])
        sl = slice(b * TB, (b + 1) * TB)
        st.dma(DR['PTd'][:, :, b * 4:(b + 1) * 4, :], PTt[:], [PTr], [('PTd', b)])
        st.dma(DR['Qd'][:, :, b * 4:(b + 1) * 4, :], Qt[:], [Qr], [('Qd', b)])
        st.dma(DR['Rpd'][:, :, sl], Rp[:], [Rr], [('Rpd', b)])
        st.dma(DR['Yld'][:, :, sl], Yl[:], [Yr], [('Yld', b)])
        st.dma(DR['bvd'][:, :, sl], bvt[:], [bvr], [('bvd', b)])
        st.dma(DR['gd'][:, :, sl], gt[:], [gr], [('gd', b)])
        if i == 0:
            st.dma(DR['vfd'][:, :, sl], vft[:], [vfr], [('vfd', b)])
    st.close()


def a2_stage(nc, C, A, DR):
    st = St(nc, C['pb']); sb = st.sb
    Sg = sb('Sg', [128, 6, 128])
    PTb = [sb('PTb%d' % k, [128, 6, 4, 64]) for k in range(2)]; Qb = [sb('Qb%d' % k, [128, 6, 4, 64]) for k in range(2)]
    for hp in range(6):
        st.memset('pool', Sg[:, hp, 0:64], 0.0, [('Sg', hp)])
        st.copy('pool', Sg[:, hp, 64:128], C['ident2'][:, 0:64], [], [('Sg', hp)])
    for b in range(NBLK):
        PTt = PTb[b % 2]; PTr = 'PTb%d' % (b % 2); Qt = Qb[b % 2]; Qr = 'Qb%d' % (b % 2)
        st.dma(PTt[:], DR['PTd'][:, :, b * 4:(b + 1) * 4, :], [], [PTr])
        st.dma(Qt[:], DR['Qd'][:, :, b * 4:(b + 1) * 4, :], [], [Qr])
        for cc in range(4):
            c = b * 4 + cc
            for hp in range(6):
                bank = hp % 4; br = 'pb%d' % bank
                for e in range(2):
                    p0 = 64 * e
                    st.mm(st.pb[bank][p0:p0 + 64, 0:128], PTt[p0:p0 + 64, hp, cc, :], Sg[p0:p0 + 64, hp, :], [PTr, ('Sg', hp)], [br])
                wc = C['WC'][:, hp, c:c + 1]
                st.stt(Sg[:, hp, 0:64], st.pb[bank][:, 0:64], wc, Qt[:, hp, cc, :], MUL, ADD, [br, Qr], [('Sg', hp)])
                st.ts('dve', Sg[:, hp, 64:128], st.pb[bank][:, 64:128], wc, None, MUL, None, [br], [('Sg', hp)])
    nc_ = st.nc
    _xc[0] += 1
    mine = nc_.dram_tensor("smine%d" % _xc[0], [128, 768], F32)
    gath = nc_.dram_tensor("sgath%d" % _xc[0], [NCORE * 128, 768], F32)
    st.dma(mine[:, :], Sg[:].rearrange("p a b -> p (a b)"), [('Sg', hp) for hp in range(6)], ['smine'])
    st.P.op('pool', lambda e: e.collective_compute("AllGather", ALU.bypass, replica_groups=[list(range(NCORE))],
                                                   ins=[mine.ap().opt()], outs=[gath.ap().opt()]),
            reads=['smine'], writes=['sgath'], kind='cc')
    G = sb('G', [128, NCORE - 1, 768])
    for j in range(NCORE - 1):
        st.dma(G[:, j, :], gath[j * 128:(j + 1) * 128, :], ['sgath'], [('G', j)])
    Ss = C['Sst']
    MT = [sb('MT%d' % k, [128, 64]) for k in range(2)]
    e1 = sb('e1', [128, 64])
    for hp in range(6):
        st.memset('pool', Ss[:, hp, :], 0.0, [('Sst', hp)])
    n = 0
    for j in range(NCORE - 1):
        for hp in range(6):
            mt = MT[n % 2]; mr = 'MT%d' % (n % 2); bank = 4 + n % 2; br = 'pb%d' % bank; n += 1
            for e in range(2):
                p0 = 64 * e
                st.mm(st.pb[bank][p0:p0 + 64, 0:64], G[p0:p0 + 64, j, hp * 128 + 64:hp * 128 + 128], C['ident'][p0:p0 + 64, p0:p0 + 64], [('G', j)], [br])
            st.copy('act', mt[:], st.pb[bank][:, 0:64], [br], [mr])
            for e in range(2):
                p0 = 64 * e
                st.mm(st.pb[bank][p0:p0 + 64, 64:128], mt[p0:p0 + 64, :], Ss[p0:p0 + 64, hp, :], [mr, ('Sst', hp)], [br])
            st.tt('dve', e1[:], st.pb[bank][:, 64:128], G[:, j, hp * 128:hp * 128 + 64], ADD, [br, ('G', j)], ['e1'])
            st.tt('dve', e1[:], e1[:], Ss[:, hp, :], SUB, ['e1', ('Sst', hp)], ['e1'])
            st.stt(Ss[:, hp, :], e1[:], C['use'][:, j:j + 1], Ss[:, hp, :], MUL, ADD, ['e1', ('Sst', hp)], [('Sst', hp)])
    st.close()


def a3_stage(nc, C, A, i, xsrc, xs, DR):
    l = i
    st = St(nc, C['pb']); sb = st.sb; PV = C['PV']; pv = C['pv']
    S = C['Sst']
    xb = [sb('xb%d' % k, [128, 8, TB]) for k in range(2)]
    PTb = [sb('PTb%d' % k, [128, 6, 4, 64]) for k in range(2)]; Qb = [sb('Qb%d' % k, [128, 6, 4, 64]) for k in range(2)]
    Rpb = [sb('Rpb%d' % k, [128, 6, TB]) for k in range(2)]; Ylb = [sb('Ylb%d' % k, [128, 6, TB]) for k in range(2)]
    bvb = [sb('bvb%d' % k, [128, 6, TB]) for k in range(2)]; gb = [sb('gb%d' % k, [128, 6, TB]) for k in range(2)]
    mob = [sb('mob%d' % k, [128, 2, TB]) for k in range(2)]
    y = sb('y', [128, 6, TB]); yc = sb('yc', [128, TB]); ysq = sb('ysq', [128, TB]); rsd = sb('rsd', [128, TB]); mix = sb('mix', [128, 6, TB])
    for b in range(NBLK):
        k2 = b % 2
        x = xb[k2]; xr_ = 'xb%d' % k2; PTt = PTb[k2]; PTr = 'PTb%d' % k2; Qt = Qb[k2]; Qr = 'Qb%d' % k2
        Rp = Rpb[k2]; Rr = 'Rpb%d' % k2; Yl = Ylb[k2]; Yr = 'Ylb%d' % k2; bvt = bvb[k2]; bvr = 'bvb%d' % k2; gt = gb[k2]; gr = 'gb%d' % k2
        mo = mob[k2]; mor = 'mob%d' % k2
        sl = slice(b * TB, (b + 1) * TB)
        st.dma(x[:], xsrc[:, :, sl], [], [xr_])
        st.dma(PTt[:], DR['PTd'][:, :, b * 4:(b + 1) * 4, :], [], [PTr]); st.dma(Qt[:], DR['Qd'][:, :, b * 4:(b + 1) * 4, :], [], [Qr])
        st.dma(Rp[:], DR['Rpd'][:, :, sl], [], [Rr]); st.dma(Yl[:], DR['Yld'][:, :, sl], [], [Yr])
        st.dma(bvt[:], DR['bvd'][:, :, sl], [], [bvr]); st.dma(gt[:], DR['gd'][:, :, sl], [], [gr]); st.dma(mo[:], DR['mod'][:, :, sl], [], [mor])
        for cc in range(4):
            c = b * 4 + cc; cs_ = slice(cc * 64, cc * 64 + 64)
            for hp in range(6):
                bank = hp % 4; br = 'pb%d' % bank
                for e in range(2):
                    p0 = 64 * e
                    st.mm(st.pb[bank][p0:p0 + 64, 0:64], S[p0:p0 + 64, hp, :], Rp[p0:p0 + 64, hp, cs_], [('Sst', hp), Rr], [br])
                    st.mm(st.pb[bank][p0:p0 + 64, 64:128], PTt[p0:p0 + 64, hp, cc, :], S[p0:p0 + 64, hp, :], [PTr, ('Sst', hp)], [br])
                st.tt('dve', y[:, hp, cs_], st.pb[bank][:, 0:64], Yl[:, hp, cs_], ADD, [br, Yr], [('y', hp)])
                st.stt(S[:, hp, :], st.pb[bank][:, 64:128], C['WC'][:, hp, c:c + 1], Qt[:, hp, cc, :], MUL, ADD, [br, Qr], [('Sst', hp)])
        for hp in range(6):
            bank = 4 + hp % 2; br = 'pb%d' % bank
            st.mm(st.pb[bank][:, 0:TB], C['bones'][:, 0:128], y[:, hp, :], [('y', hp)], [br])
            st.stt(yc[:], st.pb[bank][:, 0:TB], -1.0 / 64, y[:, hp, :], MUL, ADD, [br, ('y', hp)], ['yc'])
            st.act(ysq[:], yc[:], AF.Square, ['yc'], ['ysq'])
            st.mm(st.pb[bank][:, TB:2 * TB], C['bones'][:, 0:128], ysq[:], ['ysq'], [br])
            st.act(rsd[:], st.pb[bank][:, TB:2 * TB], AF.Sqrt, [br], ['rsd'], scale=1.0 / 64, bias=64e-5)
            st.recip(rsd[:], rsd[:], ['rsd'], ['rsd'])
            st.tt('dve', yc[:], yc[:], rsd[:], MUL, ['yc', 'rsd'], ['yc'])
            cw = PV['lnx_w'] + i * 6 + hp; cb_ = PV['lnx_b'] + i * 6 + hp
            st.ts('dve', yc[:], yc[:], pv[:, cw:cw + 1], pv[:, cb_:cb_ + 1], MUL, ADD, ['yc'], ['yc'])
            st.tt('pool', yc[:], yc[:], bvt[:, hp, :], ADD, ['yc', bvr], ['yc'])
            st.tt('pool', mix[:, hp, :], yc[:], gt[:, hp, :], MUL, ['yc', gr], [('mix', hp)])
        cat = [mix[:, c, :] for c in range(6)] + [mo[:, c, :] for c in range(2)]
        out_proj(st, C, l, A, cat, [('mix', c) for c in range(6)] + [mor], x, xr_, x, xr_)
        if b == NBLK - 1:
            st.copy('pool', C['pubx'][:], x[:, :, TB - 2:TB], [xr_], ['pubx'])
        st.dma(xs[:, :, sl], x[:], [xr_], [('xs', b)])
    st.close()


WSPEC = [('w_out', [4, D, D]), ('w_mem_kv', [4, D, 512]), ('ffn_in', [4, D, 2 * DFF]), ('ffn_out', [4, DFF, D]),
         ('a_w_in', [2, D, 2560]), ('a_w1', [2, D, 64]), ('a_w2', [2, 64, DMIX]), ('a_a1', [2, D, 64]), ('a_a2', [2, 64, DMIX]),
         ('a_g1', [2, D, 128]), ('a_g2', [2, 128, DMIX]), ('a_v1', [1, D, 32]), ('a_v2', [1, 32, DMIX]),
         ('w_kv', [D, 2 * DMIX]), ('b_w_in', [2, D, D])]

PVSPEC = [('ln1', 32), ('ln2', 32), ('conv_w', 264), ('conv_b', 88), ('mu_rkv', 36), ('mu_x', 48), ('w0', 12), ('a0', 12), ('k_k', 12),
          ('k_a', 12), ('r_k', 12), ('lnx_w', 12), ('lnx_b', 12), ('mu_v', 8), ('v0', 6), ('mem_norm', 8), ('ln_kv', 8), ('ln_f', 8)]
PVOFF = {}
_o = 0
for _n, _w in PVSPEC:
    PVOFF[_n] = _o; _o += _w
NPV = _o
CST = dict(ident=(0, 128), ones=(128, 128), bones=(256, 128), mlow=(384, 128), mask4=(512, 512), cmask=(1024, TB), ident2=(1024 + TB, 64))
NCST = 1024 + TB + 64
NLAYERS = int(os.environ.get('KLAYERS', '4'))


def build(stages=None):
    nc = bass.Bass("TRN2", target_bir_lowering=False)
    inp = lambda n, s: nc.dram_tensor(n, s, F32, kind="ExternalInput").ap()

    class Lazy(dict):
        def __missing__(self, k):
            v = inp(k, dict(WSPEC + [('biasT', [2, 12, 128, 640])])[k])
            self[k] = v
            return v
    A = Lazy()
    nc._declared = A
    A['xT'] = inp('xT', [D, TPC]); A['xhalo'] = inp('xhalo', [128, 16]); A['memT'] = inp('memT', [D, 256])
    A['cst'] = inp('cst', [128, NCST]); A['pvec'] = inp('pvec', [128, NPV]); A['selv'] = inp('selv', [128, 8]); A['usev'] = inp('usev', [128, 8])
    outT = nc.dram_tensor('outT', [D, TPC], F32, kind="ExternalOutput").ap()
    DR = {}
    for n, s in [('xs', [128, 8, TPC]), ('PTd', [128, 6, 32, 64]), ('Qd', [128, 6, 32, 64]), ('Rpd', [128, 6, TPC]), ('Yld', [128, 6, TPC]),
                 ('bvd', [128, 6, TPC]), ('gd', [128, 6, TPC]), ('vfd', [128, 6, TPC]), ('mod', [128, 2, TPC]), ('KTd', [128, 6, TPC + 512]),
                 ('Vtd', [128, 20, 832])]:
        DR[n] = nc.dram_tensor(n, s, F32).ap()
    xs = DR['xs']
    xTv = A['xT'].rearrange("(kc p) t -> p kc t", p=128)
    outv = outT.rearrange("(kc p) t -> p kc t", p=128)
    with contextlib.ExitStack() as es:
        Prog.SEM_ES = es
        Prog.GLOB = {}
        sbt = lambda n, s: es.enter_context(nc.sbuf_tensor('c_' + n, s, F32))
        C = dict(PV=PVOFF)
        C['pb'] = [es.enter_context(nc.psum_tensor('pbank%d' % k, [128, 512], F32)) for k in range(8)]
        cst = sbt('cst', [128, NCST]); C['pv'] = sbt('pv', [128, NPV]); C['sel'] = sbt('sel', [128, 8]); C['use'] = sbt('use', [128, 8])
        for n, (o, w) in CST.items():
            C[n] = cst[:, o:o + w]
        C['memn'] = sbt('memn', [128, 8, 256])
        C['kmT'] = es.enter_context(nc.sbuf_tensor('c_kmT', [128, 2, 256], BF16)); C['vm'] = es.enter_context(nc.sbuf_tensor('c_vm', [128, 2, 256], BF16))
        C['onesb'] = es.enter_context(nc.sbuf_tensor('c_onesb', [128, 128], BF16)); C['identb'] = es.enter_context(nc.sbuf_tensor('c_identb', [128, 128], BF16))
        C['WC'] = sbt('WC', [128, 6, 32]); C['Sst'] = sbt('Sst', [128, 6, 64]); C['pubx'] = sbt('pubx', [128, 8, 2])
        C['wbuf'] = [sbt('wbuf%d' % k, [128, 8, 128]) for k in range(4)]
        C['wbf'] = [es.enter_context(nc.sbuf_tensor('c_wbf%d' % k, [128, 8, 128], BF16)) for k in range(4)]
        st = St(nc, C['pb'])
        st.dma(cst[:], A['cst'], [], ['cst']); st.dma(C['pv'][:], A['pvec'], [], ['pv'])
        st.dma(C['sel'][:], A['selv'], [], ['sel']); st.dma(C['use'][:], A['usev'], [], ['use'])
        st.P.op('pool', lambda e: e.memset(C['pubx'][:], 0.0), reads=[], writes=['pubx'])
        st.close()
        st = St(nc, C['pb'])
        mt = st.sb('memt', [128, 8, 256]); sq = st.sb('sq', [128, 8, 256]); rs = st.sb('rs', [128, 256])
        st.dma(mt[:], A['memT'].rearrange("(kc p) t -> p kc t", p=128), [], ['memt'])
        rmsnorm(st, C, mt[:], 'memt', PVOFF['mem_norm'], C['memn'], 'memn', 256, sq, rs)
        st.copy('pool', C['onesb'][:], C['ones'], [], ['onesb']); st.copy('pool', C['identb'][:], C['ident'], [], ['identb'])
        st.close()
        KSTOP = int(os.environ.get('KSTOP', '99'))
        for l in range(NLAYERS):
            if KSTOP <= 0:
                break
            if l < 2 and KSTOP < 4:
                xsrc = xTv
                if not os.environ.get('KSKIPA1'):
                    a1_stage(nc, C, A, l, xsrc, DR)
                if KSTOP >= 2 and not os.environ.get('KSKIPA2'):
                    a2_stage(nc, C, A, DR)
                if os.environ.get('KSKIPA2'):
                    st = St(nc, C['pb'])
                    st.memset('pool', C['Sst'][:], 0.5, ['Sst']); st.memset('pool', C['WC'][:], 0.5, ['WC'])
                    st.close()
                if KSTOP >= 3:
                    a3_stage(nc, C, A, l, xsrc, xs, DR)
                break
            if l < 2:
                xsrc = xTv if l == 0 else xs
                a1_stage(nc, C, A, l, xsrc, DR)
                a2_stage(nc, C, A, DR)
                a3_stage(nc, C, A, l, xsrc, xs, DR)
            else:
                if l == 2:
                    kv_stage(nc, C, A, xs, DR)
                b_stage(nc, C, A, l, xs, DR)
            ffn_stage(nc, C, A, l, xs, final_out=(outv if l == NLAYERS - 1 else None), pub=(l == 0))
        if os.environ.get('KDBGOUT'):
            dbg = outv
            st = St(nc, C['pb'])
            dt_ = [st.sb('dbgt%d' % k, [128, 8, TB]) for k in range(2)]
            for b in range(NBLK):
                if os.environ.get('KDBGCONST'):
                    st.memset('pool', dt_[b % 2][:], 1.0, ['dbgt%d' % (b % 2)])
                else:
                    st.dma(dt_[b % 2][:], xs[:, :, b * TB:(b + 1) * TB], [], ['dbgt%d' % (b % 2)])
                for kc in range(8):
                    st.dma(outT[kc * 128:(kc + 1) * 128, b * TB:(b + 1) * TB], dt_[b % 2][:, kc, :], ['dbgt%d' % (b % 2)], [('dbg', b, kc)])
            st.close()
    return nc


def _cols(v):
    return np.ascontiguousarray(np.asarray(v, np.float32).reshape(-1, 128).T)


def host_prep(inputs):
    f = lambda k: np.asarray(inputs[k], np.float32)
    x = f('x')[0]; mem = f('mem')[0]
    pvec = np.zeros((128, NPV), np.float32)

    def put(name, arr, off=0):
        c = _cols(arr)
        pvec[:, PVOFF[name] + off:PVOFF[name] + off + c.shape[1]] = c
    for l in range(4):
        put('ln1', f('ln1')[l], 8 * l); put('ln2', f('ln2')[l], 8 * l)
        for j3 in range(3):
            put('conv_w', f('ffn_conv')[l, j3], l * 66 + j3 * 22)
        put('conv_b', f('ffn_conv_b')[l], l * 22)
    for i in range(2):
        for n3 in range(3):
            put('mu_rkv', f('a_mu_rkv')[i, n3], i * 18 + n3 * 6)
            put('mu_x', f('a_mu_x')[i, n3], i * 24 + n3 * 8)
        for nm, key in [('w0', 'a_w0'), ('a0', 'a_a0'), ('k_k', 'a_k_k'), ('k_a', 'a_k_a'), ('lnx_w', 'a_lnx_w'), ('lnx_b', 'a_lnx_b')]:
            put(nm, f(key)[i], i * 6)
        put('r_k', f('a_r_k')[i].reshape(-1), i * 6)
    put('mu_v', f('a_mu_v')[0]); put('v0', f('a_v0')[0])
    put('mem_norm', f('mem_norm')); put('ln_kv', f('ln_kv')); put('ln_f', f('ln_f'))
    cst = np.zeros((128, NCST), np.float32)
    I = np.eye(128, dtype=np.float32)
    p = np.arange(128)[:, None]; q = np.arange(128)[None, :]
    same = (p // 64) == (q // 64)
    cst[:, 0:128] = I; cst[:, 128:256] = 1.0; cst[:, 256:384] = same
    cst[:, 384:512] = same & (q < p)
    ups = (same & (q > p)).astype(np.float32); upi = (same & (q >= p)).astype(np.float32)
    cst[:, 512:640] = ups; cst[:, 640:768] = upi; cst[:, 768:896] = ups; cst[:, 896:1024] = upi
    cm = np.ones(TB, np.float32); cm[0::64] = 0.0
    cst[:, 1024:1024 + TB] = cm[None, :]
    cst[:, 1024 + TB:1024 + TB + 64] = np.concatenate([np.eye(64), np.eye(64)], 0)
    rel = f('b_rel')
    kt = np.arange(5)[None, :, None]; pp = np.arange(128)[:, None, None]; qq = np.arange(128)[None, None, :]
    jj = kt * 128 + pp
    cq = qq // 64
    valid = (jj >= 64 * cq) & (jj < 64 * cq + 576)
    dist = qq + 512 - jj
    idx = np.clip(dist, -63, 256) + 63
    bias = rel[:, :, idx]
    bias = np.where(valid[None, None], bias, np.float32(-1e4)).astype(np.float32)
    biasT = np.ascontiguousarray(bias.reshape(2, 12, 128, 640))
    common = dict(cst=cst, pvec=pvec, memT=np.ascontiguousarray(mem.T), biasT=biasT)
    for n, s in WSPEC:
        common[n] = np.ascontiguousarray(f(n))
    in_maps = []
    for c in range(NCORE):
        m = dict(common)
        m['xT'] = np.ascontiguousarray(x[c * TPC:(c + 1) * TPC].T)
        xh_ = x[c * TPC - 2:c * TPC].T if c > 0 else np.zeros((D, 2), np.float32)
        m['xhalo'] = np.ascontiguousarray(xh_.reshape(8, 128, 2).transpose(1, 0, 2).reshape(128, 16))
        sel = np.zeros((128, 8), np.float32); use = np.zeros((128, 8), np.float32)
        if c > 0:
            sel[:, c - 1] = 1.0
        use[:, :c] = 1.0
        m['selv'] = sel; m['usev'] = use
        in_maps.append(m)
    return in_maps


def filter_maps(nc, in_maps):
    base = ('xT', 'xhalo', 'memT', 'cst', 'pvec', 'selv', 'usev')
    keep = set(base) | set(nc._declared.keys())
    return [{k: v for k, v in m.items() if k in keep} for m in in_maps]


def kernel(**inputs):
    in_maps = host_prep(inputs)
    nc = build()
    in_maps = filter_maps(nc, in_maps)
    res = run_bass_kernel_spmd(nc, in_maps, core_ids=list(range(NCORE)))
    out = np.concatenate([np.asarray(r['outT']).T for r in res.results], axis=0)
    return np.ascontiguousarray(out[None].astype(np.float32))
```
